# Optimizing a Trainium2 kernel written in Bass

```python
import jax, jax.numpy as jnp
from jax import lax
import numpy as np

D_MODEL = 2048
BATCH = 4
SEQ = 2048
DEPTH = 2
DEC_BATCH = 128
DEC_SEQ = 1
PAST_LEN = 8192
PAGE_SIZE = 128

BRANCH_W = D_MODEL // 2
HEAD_DIM = 64
N_HEADS = BRANCH_W // HEAD_DIM
N_KV_HEADS = 4
GQA_GROUP = N_HEADS // N_KV_HEADS
KV_W = N_KV_HEADS * HEAD_DIM
WINDOW = 128
D_CONV = BRANCH_W
CONV_WIDTH = 3
D_GMLP = BRANCH_W
CHUNK = 128
N_SPATIAL_GROUPS = 8
SPATIAL_GROUP_W = D_GMLP // N_SPATIAL_GROUPS
N_BRANCHES = 3
D_FF = -(-8 * D_MODEL // (3 * 256)) * 256
COL_SIZES = (BRANCH_W, KV_W, KV_W, D_CONV, D_CONV, D_CONV, D_GMLP, D_GMLP, N_BRANCHES * D_MODEL)
IN_COLS = sum(COL_SIZES)
EPS = 1e-6
NEG_INF = -1e30

kernel_name = "hybrid_swa_shortconv_gmlp_gated_decoder_step"


def rms_norm(x, g):
    xf = x.astype(jnp.float32)
    y = xf * lax.rsqrt(jnp.mean(xf * xf, axis=-1, keepdims=True) + EPS)
    return (y * g.astype(jnp.float32)).astype(x.dtype)


def alibi_slopes():
    return jnp.exp2(-8.0 * jnp.arange(1, N_HEADS + 1, dtype=jnp.float32) / N_HEADS)


def split_cols(z):
    idx = np.cumsum(np.array(COL_SIZES))[:-1].tolist()
    return jnp.split(z, idx, axis=-1)


def window_attend(q, k, v, q_pos, k_pos, sinks):
    Bn, N, Tq = q.shape[:3]
    qg = q.reshape(Bn, N, Tq, N_KV_HEADS, GQA_GROUP, HEAD_DIM)
    s = jnp.einsum("bnqkgd,bnskd->bnkgqs", qg, k, preferred_element_type=jnp.float32) * (HEAD_DIM ** -0.5)
    dist = q_pos[:, :, None] - k_pos[:, None, :]
    valid = (k_pos[:, None, :] >= 0) & (dist >= 0) & (dist < WINDOW)
    slopes = alibi_slopes().reshape(N_KV_HEADS, GQA_GROUP)
    bias = -slopes[None, :, :, None, None] * dist.astype(jnp.float32)[:, None, None]
    s = jnp.where(valid[:, None, None], s + bias, NEG_INF)
    sink = sinks.astype(jnp.float32).reshape(N_KV_HEADS, GQA_GROUP)[None, None, :, :, None, None]
    sink = jnp.broadcast_to(sink, s.shape[:-1] + (1,))
    p = jax.nn.softmax(jnp.concatenate([s, sink], axis=-1), axis=-1)[..., :-1]
    o = jnp.einsum("bnkgqs,bnskd->bnqkgd", p.astype(v.dtype), v)
    return o.reshape(Bn, N, Tq, BRANCH_W)


def prompt_window_attention(q, k, v, sinks):
    Bn, T = q.shape[:2]
    nb = T // WINDOW
    qb = q.reshape(Bn, nb, WINDOW, N_HEADS, HEAD_DIM)

    def band(xb):
        prev = jnp.concatenate([jnp.zeros_like(xb[:, :1]), xb[:, :-1]], axis=1)
        return jnp.concatenate([prev, xb], axis=2)

    kb = band(k.reshape(Bn, nb, WINDOW, N_KV_HEADS, HEAD_DIM))
    vb = band(v.reshape(Bn, nb, WINDOW, N_KV_HEADS, HEAD_DIM))
    pos = jnp.arange(T, dtype=jnp.int32).reshape(nb, WINDOW)
    k_pos = jnp.concatenate([pos - WINDOW, pos], axis=1)
    return window_attend(qb, kb, vb, pos, k_pos, sinks).reshape(Bn, T, BRANCH_W)


def sample_window_attention(q, k, v, k_buf, v_buf, sinks):
    Bn, T = q.shape[:2]
    n_buf = k_buf.shape[1]
    k_all = jnp.concatenate([k_buf, k], axis=1)[:, None]
    v_all = jnp.concatenate([v_buf, v], axis=1)[:, None]
    q_pos = PAST_LEN + jnp.arange(T, dtype=jnp.int32)
    k_pos = jnp.concatenate([PAST_LEN - n_buf + jnp.arange(n_buf, dtype=jnp.int32), q_pos])
    o = window_attend(q[:, None], k_all, v_all, q_pos[None], k_pos[None], sinks)
    return o.reshape(Bn, T, BRANCH_W)


def short_conv(z_pad, w, T):
    return sum(w[j] * z_pad[:, j:j + T] for j in range(CONV_WIDTH))


def spatial_mix(v, w_s, b_s):
    T = v.shape[2]
    mask = jnp.tril(jnp.ones((T, T), dtype=bool))
    w = jnp.where(mask[None], w_s[:, :T, :T], jnp.zeros((), w_s.dtype))
    return jnp.einsum("gpq,bnqgc->bnpgc", w, v) + b_s[:, :T].T[None, None, :, :, None]


def layer(x, lw, is_prompt, k_buf=None, v_buf=None, conv_buf=None):
    (norm_mix, w_in, b_gate, q_norm, k_norm, sinks, conv_w, v_norm,
     w_spatial, b_spatial, w_branch, w_out, norm_ffn, w_gate_up, w_down) = lw
    Bn, T, _ = x.shape
    xn = rms_norm(x, norm_mix)
    q, k, v, bg, cg, h, u, vg, g = split_cols(xn @ w_in)

    q = rms_norm(q.reshape(Bn, T, N_HEADS, HEAD_DIM), q_norm)
    k = rms_norm(k.reshape(Bn, T, N_KV_HEADS, HEAD_DIM), k_norm)
    v = v.reshape(Bn, T, N_KV_HEADS, HEAD_DIM)
    if is_prompt:
        o_a = prompt_window_attention(q, k, v, sinks)
        keep = min(WINDOW, T)
        new_k, new_v = k[:, T - keep:], v[:, T - keep:]
    else:
        o_a = sample_window_attention(q, k, v, k_buf, v_buf, sinks)
        new_k, new_v = k, v

    zc = cg * h
    if is_prompt:
        z_pad = jnp.concatenate([jnp.zeros((Bn, CONV_WIDTH - 1, D_CONV), zc.dtype), zc], axis=1)
    else:
        z_pad = jnp.concatenate([conv_buf, zc], axis=1)
    o_b = bg * short_conv(z_pad, conv_w, T)
    new_conv = z_pad[:, -(CONV_WIDTH - 1):]

    u = jax.nn.gelu(u)
    vg = rms_norm(jax.nn.gelu(vg), v_norm)
    n_chunks = T // CHUNK if is_prompt else 1
    vc = vg.reshape(Bn, n_chunks, T // n_chunks, N_SPATIAL_GROUPS, SPATIAL_GROUP_W)
    o_c = u * spatial_mix(vc, w_spatial, b_spatial).reshape(Bn, T, D_GMLP)

    branches = jnp.stack([o_a, o_b, o_c], axis=2)
    proj = jnp.einsum("btic,icd->btid", branches, w_branch)
    gates = jax.nn.sigmoid(g.reshape(Bn, T, N_BRANCHES, D_MODEL) + b_gate)
    x = x + jnp.sum(gates * proj, axis=2) @ w_out

    gate, up = jnp.split(rms_norm(x, norm_ffn) @ w_gate_up, 2, axis=-1)
    x = x + (jax.nn.silu(gate) * up) @ w_down
    return x, new_k, new_v, new_conv, vg


def setup_inputs(seed: int = 0) -> dict:
    key = jax.random.key(seed)
    ks = jax.random.split(key, 24)
    f32 = jnp.float32
    n = lambda k, shape, s: jax.random.normal(k, shape, f32) * s
    w_buf = min(WINDOW, PAST_LEN)
    return {
        "x_prompt": n(ks[0], (BATCH, SEQ, D_MODEL), 1.0),
        "x_sample": n(ks[1], (DEC_BATCH, DEC_SEQ, D_MODEL), 1.0),
        "cache_k": n(ks[2], (DEPTH, DEC_BATCH, w_buf, N_KV_HEADS, HEAD_DIM), 1.0),
        "cache_v": n(ks[3], (DEPTH, DEC_BATCH, w_buf, N_KV_HEADS, HEAD_DIM), 1.0),
        "state_conv": n(ks[4], (DEPTH, DEC_BATCH, CONV_WIDTH - 1, D_CONV), 1.0),
        "norm_mix": 1.0 + n(ks[5], (DEPTH, D_MODEL), 0.05),
        "w_in": n(ks[6], (DEPTH, D_MODEL, IN_COLS), D_MODEL ** -0.5),
        "b_gate": n(ks[7], (DEPTH, N_BRANCHES, D_MODEL), 0.1),
        "q_norm": 1.0 + n(ks[8], (DEPTH, HEAD_DIM), 0.05),
        "k_norm": 1.0 + n(ks[9], (DEPTH, HEAD_DIM), 0.05),
        "sinks": n(ks[10], (DEPTH, N_HEADS), 0.5),
        "conv_w": n(ks[11], (DEPTH, CONV_WIDTH, D_CONV), CONV_WIDTH ** -0.5),
        "v_norm": 1.0 + n(ks[12], (DEPTH, D_GMLP), 0.05),
        "w_spatial": n(ks[13], (DEPTH, N_SPATIAL_GROUPS, CHUNK, CHUNK), CHUNK ** -0.5),
        "b_spatial": 1.0 + n(ks[14], (DEPTH, N_SPATIAL_GROUPS, CHUNK), 0.1),
        "w_branch": n(ks[15], (DEPTH, N_BRANCHES, BRANCH_W, D_MODEL), BRANCH_W ** -0.5),
        "w_out": n(ks[16], (DEPTH, D_MODEL, D_MODEL), D_MODEL ** -0.5),
        "norm_ffn": 1.0 + n(ks[17], (DEPTH, D_MODEL), 0.05),
        "w_gate_up": n(ks[18], (DEPTH, D_MODEL, 2 * D_FF), D_MODEL ** -0.5),
        "w_down": n(ks[19], (DEPTH, D_FF, D_MODEL), D_FF ** -0.5),
    }


def reference(x_prompt, x_sample, cache_k, cache_v, state_conv, norm_mix, w_in, b_gate, q_norm, k_norm,
              sinks, conv_w, v_norm, w_spatial, b_spatial, w_branch, w_out, norm_ffn, w_gate_up, w_down):
    weights = (norm_mix, w_in, b_gate, q_norm, k_norm, sinks, conv_w, v_norm,
               w_spatial, b_spatial, w_branch, w_out, norm_ffn, w_gate_up, w_down)
    xp, xs = x_prompt, x_sample
    kp, vp, cp, ksm, vsm, csm, gsm = [], [], [], [], [], [], []
    for l in range(DEPTH):
        lw = tuple(a[l] for a in weights)
        xp, k1, v1, c1, _ = layer(xp, lw, True)
        xs, k2, v2, c2, g2 = layer(xs, lw, False, cache_k[l], cache_v[l], state_conv[l])
        kp.append(k1); vp.append(v1); cp.append(c1)
        ksm.append(k2); vsm.append(v2); csm.append(c2); gsm.append(g2)
    return (xp, xs, jnp.stack(kp), jnp.stack(vp), jnp.stack(cp),
            jnp.stack(ksm), jnp.stack(vsm), jnp.stack(csm), jnp.stack(gsm))
```

```python
import numpy as np
import concourse.bass as bass
import concourse.mybir as mybir
from concourse.bass_utils import run_bass_kernel_spmd

F32 = mybir.dt.float32
BF16 = mybir.dt.bfloat16
AF = mybir.ActivationFunctionType
ALU = mybir.AluOpType
AX = mybir.AxisListType

D = 2048
L = 2
DFF = 5632
INC = 12800
NEG = -30000.0
EPS = 1e-6
GC = 1.5957691216057308

ENG = {"pe": "tensor", "act": "scalar", "dve": "vector", "pool": "gpsimd", "sp": "sync"}


class Res:
    __slots__ = ("w", "r")

    def __init__(self):
        self.w = None
        self.r = {}


class Agent:
    def __init__(self, name, sem, inc):
        self.name, self.sem, self.inc, self.count = name, sem, inc, 0


class Rec:
    def __init__(self, sems):
        self.sems = list(sems)
        self.ops = {e: [] for e in ENG}
        self.ag = {e: Agent(e, self.sems.pop(), 1) for e in ENG}
        self.known = {e: {} for e in ENG}
        self.dma_agents = []

    def dma_agent(self, name):
        a = Agent(name, self.sems.pop(), 16)
        self.dma_agents.append(a)
        return a

    def _need(self, eng, dep):
        if dep is None:
            return
        a, c = dep
        if c <= 0:
            return
        if a.name == eng and eng == "pe":
            return
        if self.known[eng].get(a.name, 0) >= c:
            return
        assert c <= a.count, ("wait on unsignalled op", eng, a.name, c, a.count)
        self.known[eng][a.name] = c
        sem, val = a.sem, c * a.inc
        self.ops[eng].append(lambda e, sem=sem, val=val: e.wait_ge(sem, val))

    def _deps(self, eng, reads, writes):
        for r in reads:
            self._need(eng, r.w)
        for r in writes:
            self._need(eng, r.w)
            for a, c in r.r.items():
                self._need(eng, (a, c))

    def op(self, eng, fn, reads=(), writes=(), signal=True):
        self._deps(eng, reads, writes)
        a = self.ag[eng]
        if signal:
            a.count += 1
            sem = a.sem
            self.ops[eng].append(lambda e, fn=fn, sem=sem: fn(e).then_inc(sem, 1))
            me = (a, a.count)
        else:
            self.ops[eng].append(lambda e, fn=fn: fn(e))
            me = (a, a.count + 1)
        for r in reads:
            r.r[a] = max(r.r.get(a, 0), me[1])
        for r in writes:
            r.w = me
            r.r = {}

    def dma(self, q, agent, out, in_, reads=(), writes=()):
        pairs = out if in_ is None else [(out, in_)]
        self._deps(q, reads, writes)
        self._need(q, (agent, agent.count))
        sem = agent.sem
        for (o_, i_) in pairs:
            agent.count += 1
            self.ops[q].append(lambda e, out=o_, in_=i_, sem=sem: e.dma_start(out=out, in_=in_).then_inc(sem, 16))
        me = (agent, agent.count)
        for r in reads:
            r.r[agent] = max(r.r.get(agent, 0), me[1])
        for r in writes:
            r.w = me
            r.r = {}

    def barrier(self):
        allag = list(self.ag.values()) + self.dma_agents
        for e in ENG:
            for a in allag:
                if a.name != e:
                    self._need(e, (a, a.count))
                elif e != "pe":
                    self._need(e, (a, a.count))


def chunks(a, b):
    w = b - a
    if w <= 512:
        return [(a, b)]
    h = (w // 2 + 1) // 2 * 2
    return [(a, a + h), (a + h, b)]


class Stop(Exception):
    pass


def build(stop=None):
    nc = bass.Bass("TRN2", target_bir_lowering=False)

    def ckpt(name):
        if stop == name:
            raise Stop()

    dt = nc.dram_tensor

    def din(name, shape):
        return dt(name, list(shape), F32, kind="ExternalInput").ap()

    def dout(name, shape):
        return dt(name, list(shape), F32, kind="ExternalOutput").ap()

    xin = din("xin", [1280, D])
    xsm = din("xsm", [16, D])
    ck = din("ck", [L, 16, 128, 4, 64])
    cv = din("cv", [L, 16, 128, 4, 64])
    sconv = din("sconv", [L, 32, 1024])
    w_in = din("w_in", [L, D, INC])
    w_br = din("w_br", [L, 3, 1024, D])
    w_out = din("w_out", [L, D, D])
    w_gu = din("w_gu", [L, D, 2 * DFF])
    w_dn = din("w_dn", [L, DFF, D])
    pp = din("pp", [L, 128, 144])
    vnb = din("vnb", [L, 128, 1024])
    wsp = din("wsp", [L, 128, 8, 128])
    bsr = din("bsr", [L, 1152])
    cst = din("cst", [128, 512])
    btab = din("btab", [2, 128, 16, 128])
    bsm = din("bsm", [128, 512])

    y_p = dout("y_p", [1024, D])
    y_s = dout("y_s", [16, D])
    nk_p = dout("nk_p", [L, 128, 256])
    nv_p = dout("nv_p", [L, 128, 256])
    nc_p = dout("nc_p", [L, 2, 1024])
    nk_s = dout("nk_s", [L, 16, 256])
    nv_s = dout("nv_s", [L, 16, 256])
    nc_s = dout("nc_s", [L, 16, 2, 1024])
    ng_s = dout("ng_s", [L, 16, 1024])
    if stop is not None:
        dbg_x = dout("dbg_x", [128, 16, 656])
        dbg_xn = dout("dbg_xn", [128, 16, 656])
        dbg_ar = dout("dbg_ar", [128, 28864])
        dbg_kd = dout("dbg_kd", [128, 4, 784])
        dbg_vd = dout("dbg_vd", [128, 7, 4, 128])

    from contextlib import ExitStack
    es = ExitStack()
    with es:
        def sb(name, shape, dtype):
            return es.enter_context(nc.sbuf_tensor(name, list(shape), dtype))

        X = sb("X", [128, 16, 656], F32)
        XN = sb("XN", [128, 16, 656], BF16)
        WB = [sb(f"WB{i}", [128, 4096], BF16) for i in range(3)]
        AR = sb("AR", [128, 28864], BF16)
        BT = sb("BT", [128, 2, 16, 128], BF16)
        KD = sb("KD", [128, 4, 784], BF16)
        VD = sb("VD", [128, 7, 4, 128], BF16)
        KS = sb("KS", [128, L, 4, 128], BF16)
        VS = sb("VS", [128, L, 4, 128], BF16)
        ZS = sb("ZS", [128, L, 8, 2], BF16)
        CST = sb("CST", [128, 512], F32)
        CB = sb("CB", [128, 384], BF16)
        BSM = sb("BSM", [128, 512], F32)
        PP = sb("PP", [128, 144], F32)
        GQ8 = sb("GQ8", [128, 1], F32)
        ESK = sb("ESK", [128, 16], F32)
        VNB = sb("VNB", [128, 1024], F32)
        WST = sb("WST", [128, 8, 128], BF16)
        DG = sb("DG", [16, 8, 16], BF16)
        BSRB = sb("BSRB", [1, 1152], BF16)
        STG = sb("STG", [128, 1024], F32)
        STGA = AR[:, 0:4096].bitcast(F32)
        STGS = [STGA, AR[:, 4096:8192].bitcast(F32), AR[:, 8192:12288].bitcast(F32)]
        T1 = sb("T1", [128, 512], F32)
        T2 = sb("T2", [128, 512], F32)
        T3 = sb("T3", [128, 544], F32)
        ST = sb("ST", [128, 4], F32)
        TB = [sb(f"TB{i}", [128, 512], BF16) for i in range(2)]
        KOF = sb("KOF", [128, 4, 144], F32)
        VOF = sb("VOF", [128, 2, 256], F32)
        SCT = sb("SCT", [128, 8, 32], F32)
        CKD = [sb(f"CKD{i}", [128, 4, 128], BF16) for i in range(2)]
        CVD = [sb(f"CVD{i}", [128, 4, 128], BF16) for i in range(2)]
        KCT = [sb(f"KCT{i}", [128, 4, 128], BF16) for i in range(1)] * 2
        PS = [es.enter_context(nc.psum_tensor(f"PS{i}", [128, 512], F32)) for i in range(8)]
        sems = [es.enter_context(nc.semaphore(f"s{i}")) for i in range(48)]

        R = Rec(sems)
        rX = [Res() for _ in range(16)]
        rXN = [Res() for _ in range(16)]
        rWB = [Res() for _ in range(3)]
        aWB = [R.dma_agent(f"wb{i}") for i in range(3)]
        rPS = [Res() for _ in range(8)]
        rAR = {}
        rM = {}

        def res(name):
            if name not in rM:
                rM[name] = Res()
            return rM[name]

        a_in = R.dma_agent("in")
        a_in2 = R.dma_agent("in2")
        a_ins = [R.dma_agent("ins0"), R.dma_agent("ins1")]
        a_out = R.dma_agent("out")
        a_out2 = R.dma_agent("out2")
        a_ck = [R.dma_agent("ck0"), R.dma_agent("ck1")]
        a_cv = [R.dma_agent("cv0"), R.dma_agent("cv1")]
        out_agents = [a_out, a_out2]

        ident = CST[:, 0:128]
        maskqp = CST[:, 384:512]
        ones_f = CST[:, 256:384]
        ident_bf = CB[:, 0:128]
        blk_bf = CB[:, 128:256]
        ones_bf = CB[:, 256:384]

        psn = [0]

        pinned = set()

        def bank(pin=False):
            while True:
                i = psn[0] % 8
                psn[0] += 1
                if i not in pinned:
                    break
            if pin:
                pinned.add(i)
            return PS[i], rPS[i]

        def bank_at(i):
            return PS[i], rPS[i]

        def mm(out, lhsT, rhs, start, stop, reads, writes, signal, skip=False):
            R.op("pe", lambda e: e.matmul(out, lhsT, rhs, start=start, stop=stop, skip_group_check=skip),
                 reads, writes, signal)

        def tr(out, in_, idn, reads, writes, signal=True):
            R.op("pe", lambda e: e.transpose(out, in_, idn), reads, writes, signal)

        def act(out, in_, func, reads, writes, bias=None, scale=None):
            kw = {}
            if bias is not None:
                kw["bias"] = bias
            if scale is not None:
                kw["scale"] = scale
            R.op("act", lambda e: e.activation(out=out, in_=in_, func=func, **kw), reads, writes)

        def tt(eng, out, in0, in1, op, reads, writes):
            R.op(eng, lambda e: e.tensor_tensor(out=out, in0=in0, in1=in1, op=op), reads, writes)

        def ts(eng, out, in0, s1, s2, op0, op1, reads, writes):
            if op1 is None:
                R.op(eng, lambda e: e.tensor_scalar(out=out, in0=in0, scalar1=s1, scalar2=None, op0=op0), reads, writes)
            else:
                R.op(eng, lambda e: e.tensor_scalar(out=out, in0=in0, scalar1=s1, scalar2=s2, op0=op0, op1=op1), reads, writes)

        def stt(out, in0, scalar, in1, op0, op1, reads, writes):
            R.op("dve", lambda e: e.scalar_tensor_tensor(out=out, in0=in0, scalar=scalar, in1=in1, op0=op0, op1=op1),
                 reads, writes)

        def cp(eng, out, in_, reads, writes):
            if eng == "act":
                R.op("act", lambda e: e.copy(out=out, in_=in_), reads, writes)
            else:
                R.op(eng, lambda e: e.tensor_copy(out=out, in_=in_), reads, writes)

        slabs = []
        state = {"issued": 0, "used": 0}

        def issue(i):
            d = slabs[i]
            slot = i % 3
            wb = WB[slot]
            kind = d[0]
            l = d[1]
            if kind == "in":
                _, _, c0, ncol = d
                dst = wb[:, 0:16 * ncol].rearrange("p (k c) -> p k c", c=ncol)
                src = w_in[l].rearrange("(k p) c -> p k c", p=128)[:, :, c0:c0 + ncol]
                R.dma("pool", aWB[slot], dst, src, (), (rWB[slot],))
            elif kind == "kd":
                _, _, kp = d
                dst = wb[:, 0:4096].rearrange("p (k v u d) -> p k v u d", k=16, v=2, u=2)
                src = w_in[l].rearrange("(k p) c -> p k c", p=128)[:, :, 1024 + kp * 128:1024 + kp * 128 + 128]
                src = src.rearrange("p k (v d) -> p k v d", v=2)
                R.dma("pool", aWB[slot], [(dst[:, :, v, u, :], src[:, :, v, :]) for u in range(2) for v in range(2)], None, (), (rWB[slot],))
            elif kind == "wb":
                _, _, i_br, c0 = d
                dst = wb[:, 0:2048].rearrange("p (k c) -> p k c", c=256)
                src = w_br[l, i_br].rearrange("(k p) c -> p k c", p=128)[:, :, c0:c0 + 256]
                R.dma("pool", aWB[slot], dst, src, (), (rWB[slot],))
            elif kind == "out":
                _, _, c0 = d
                dst = wb[:, 0:4096].rearrange("p (k c) -> p k c", c=256)
                src = w_out[l].rearrange("(k p) c -> p k c", p=128)[:, :, c0:c0 + 256]
                R.dma("pool", aWB[slot], dst, src, (), (rWB[slot],))
            elif kind == "gu":
                _, _, m = d
                wsrc = w_gu[l].rearrange("(k p) c -> p k c", p=128)
                prs = [(wb[:, 0:2048].rearrange("p (k c) -> p k c", c=128), wsrc[:, :, m * 128:(m + 1) * 128]),
                       (wb[:, 2048:4096].rearrange("p (k c) -> p k c", c=128), wsrc[:, :, DFF + m * 128:DFF + (m + 1) * 128])]
                R.dma("pool", aWB[slot], prs, None, (), (rWB[slot],))
            elif kind == "gp":
                _, _, i_br, t = d
                c0 = 6656 + i_br * 2048 + t * 128
                prs = [(wb[:, 0:2048].rearrange("p (k c) -> p k c", c=128),
                        w_in[l].rearrange("(k p) c -> p k c", p=128)[:, :, c0:c0 + 128]),
                       (wb[:, 2048:3072].rearrange("p (k c) -> p k c", c=128),
                        w_br[l, i_br].rearrange("(k p) c -> p k c", p=128)[:, :, t * 128:(t + 1) * 128])]
                R.dma("pool", aWB[slot], prs, None, (), (rWB[slot],))
            elif kind == "dn":
                _, _, hf, ct = d
                dst = wb[:, 0:2816].rearrange("p (k c) -> p k c", c=128)
                src = w_dn[l].rearrange("(k p) c -> p k c", p=128)[:, hf * 22:(hf + 1) * 22, ct * 128:(ct + 1) * 128]
                R.dma("pool", aWB[slot], dst, src, (), (rWB[slot],))

        def get(desc):
            i = state["used"]
            assert slabs[i] == desc, (slabs[i], desc)
            while state["issued"] < min(len(slabs), i + 3):
                issue(state["issued"])
                state["issued"] += 1
            state["used"] += 1
            slot = i % 3
            return WB[slot], rWB[slot]

        def layer_slabs(l):
            s = []
            for j in range(4):
                s.append(("in", l, j * 256, 256))
            s.append(("kd", l, 0))
            s.append(("kd", l, 1))
            s.append(("in", l, 1280, 256))
            for i_br, seq in ((0, ()), (1, (2560, 3584, 1536)), (2, (4608, 5632))):
                for base in seq:
                    for j in range(4):
                        s.append(("in", l, base + j * 256, 256))
                for t in range(16):
                    s.append(("gp", l, i_br, t))
            for j in range(8):
                s.append(("out", l, j * 256))
            for m in range(44):
                s.append(("gu", l, m))
            for ct in range(16):
                s.append(("dn", l, 0, ct))
                s.append(("dn", l, 1, ct))
            return s

        for mt in range(2):
            for l in range(L):
                slabs.extend(layer_slabs(l))

        R.dma("sp", a_in, CST[:], cst[:, :], (), (res("cst"),))
        R.dma("sp", a_in, BSM[:], bsm[:, :], (), (res("bsm"),))
        R.dma("pool", a_in2, BT[:], btab.rearrange("t p h q -> p t h q"), (), (res("bt"),))
        cp("dve", CB[:, 0:384], CST[:, 0:384], (res("cst"),), (res("cb"),))
        rC = [res("cst"), res("cb"), res("bsm"), res("bt")]

        MTS = [
            dict(T=640, nsamp=0, layers=[0, 1], lay={0: dict(kv0=0, f0=128), 1: dict(kv0=128, f0=256)}, own0=256, first_lim=384,
                 rows0=0, yrow0=0, use_saved={0: False, 1: False}, save=True, kvonly=False),
            dict(T=656, nsamp=16, layers=[0, 1], lay={0: dict(kv0=0, f0=0), 1: dict(kv0=0, f0=0)}, own0=0, first_lim=-1,
                 rows0=640, yrow0=384, use_saved={0: True, 1: True}, save=False, kvonly=False),
        ]
        ostate = {"n": 0}

        def odma(out, in_, reads):
            ag = out_agents[ostate["n"] % 2]
            ostate["n"] += 1
            R.dma("sp", ag, out, in_, reads, ())

        def rmsnorm(l, gcol0, a, b):
            for (c0, c1) in chunks(a, b):
                n = c1 - c0
                pb, rb = bank()
                for k in range(16):
                    tb = AR[:, (k % 8) * 512:(k % 8 + 1) * 512]
                    rtb = res(f"tbx{k % 8}")
                    act(tb[:, 0:n], X[:, k, c0:c1], AF.Square, (rX[k],), (rtb,))
                    mm(pb[:, 0:n], ones_bf, tb[:, 0:n], k == 0, k == 15, (rtb, res("cb")), (rb,), True)
                act(T1[:, 0:n], pb[:, 0:n], AF.Ln, (rb,), (res("t1"),), bias=EPS, scale=1.0 / D)
                act(T1[:, 0:n], T1[:, 0:n], AF.Exp, (res("t1"),), (res("t1"),), scale=-0.5)
                for k in range(16):
                    stt(XN[:, k, c0:c1], X[:, k, c0:c1], PP[:, gcol0 + k:gcol0 + k + 1], T1[:, 0:n],
                        ALU.mult, ALU.mult, (rX[k], res("t1"), res("pp")), (rXN[k],))

        def gelu_from_psum(ps_ap, out_ap, n, P, rb, wres):
            t2 = T2[0:P, 0:n]
            act(t2, ps_ap, AF.Square, (rb,), (res("t2"),))
            ts("dve", t2, t2, 0.044715, 1.0, ALU.mult, ALU.add, (res("t2"),), (res("t2"),))
            tt("dve", t2, t2, ps_ap, ALU.mult, (res("t2"), rb), (res("t2"),))
            act(t2, t2, AF.Sigmoid, (res("t2"),), (res("t2"),), scale=GC)
            tt("dve", out_ap, t2, ps_ap, ALU.mult, (res("t2"), rb), (wres,))

        def main_body():
          for mi, M in enumerate(MTS):
            T = M["T"]
            ns = M["nsamp"]
            Tp = T - ns
            R.barrier()
            nblk = Tp // 128
            for bi in range(nblk):
                row0 = M["rows0"] + bi * 128
                sg, rsg, asg = STGS[bi % 2], res(f"stga{bi % 2}"), a_ins[bi % 2]
                R.dma("sp", asg, sg[:, :], xin[row0:row0 + 128, :], (), (rsg,))
                for q4 in range(4):
                    pb, rb = bank()
                    for j in range(4):
                        k = q4 * 4 + j
                        tr(pb[:, j * 128:(j + 1) * 128], sg[:, k * 128:(k + 1) * 128], ident,
                           (rsg, res("cst")), (rb,), j == 3)
                    cp("act" if q4 % 2 else "dve", X[:, q4 * 4:q4 * 4 + 4, bi * 128:(bi + 1) * 128],
                       pb[:, :].rearrange("p (j c) -> p j c", c=128), (rb,), tuple(rX[q4 * 4:q4 * 4 + 4]))
            if ns:
                R.dma("sp", a_in, STGA[0:16, :], xsm[:, :], (), (res("stga0"),))
                for q4 in range(4):
                    pb, rb = bank()
                    for j in range(4):
                        k = q4 * 4 + j
                        tr(pb[:, j * 16:(j + 1) * 16], STGA[0:16, k * 128:(k + 1) * 128], ident[0:16, 0:16],
                           (res("stga0"), res("cst")), (rb,), j == 3)
                    cp("dve", X[:, q4 * 4:q4 * 4 + 4, Tp:T],
                       pb[:, 0:64].rearrange("p (j c) -> p j c", c=16), (rb,), tuple(rX[q4 * 4:q4 * 4 + 4]))

            ckpt(f"xload{mi}")
            for l in M["layers"]:
                kv0 = M["lay"][l]["kv0"]
                f0 = M["lay"][l]["f0"]
                Tf = T - f0
                R.barrier()
                QN = AR[:, 0:8 * 656].rearrange("p (t c) -> p t c", c=656)
                OB = AR[:, 5248:5248 + 8 * 656].rearrange("p (t c) -> p t c", c=656)
                MG = AR[:, 10496:10496 + 16 * 656].rearrange("p (t c) -> p t c", c=656)
                ZC = AR[:, 20992:20992 + 8 * 658].rearrange("p (t c) -> p t c", c=658)
                VGN = AR[:, 20992:20992 + 6 * 1024].rearrange("p (b c) -> p b c", c=1024)
                PT = [AR[:, 27200:27200 + 512], AR[:, 27712:27712 + 512]]
                HT = AR[:, 0:44 * 656].rearrange("p (m c) -> p m c", c=656)
                rQN, rOB, rMG, rZC = res("qn"), res("ob"), [res(f"mg{t}") for t in range(16)], res("zc")
                rPT = [res("pt0"), res("pt1")]

                R.dma("sp", a_in, PP[:], pp[l], (), (res("pp"),))
                R.dma("sp", a_in, VNB[:], vnb[l], (), (res("vnb"),))
                R.dma("sp", a_in, STG[:, :].rearrange("p (g q) -> p g q", q=128), wsp[l], (), (res("stg"),))
                R.dma("pool", a_in2, BSRB[:], bsr[l:l + 1, :], (), (res("bsrb"),))
                ts("dve", GQ8[:], PP[:, 80:81], 0.125, None, ALU.mult, None, (res("pp"),), (res("gq8"),))
                act(ESK[:], PP[:, 108:124], AF.Exp, (res("pp"),), (res("esk"),))
                for g2 in range(2):
                    pb, rb = bank()
                    for j in range(4):
                        g = g2 * 4 + j
                        tr(pb[:, j * 128:(j + 1) * 128], STG[:, g * 128:(g + 1) * 128], ident, (res("stg"), res("cst")), (rb,), j == 3)
                    tt("dve", WST[:, g2 * 4:g2 * 4 + 4, :], pb[:, :].rearrange("p (j c) -> p j c", c=128),
                       maskqp.unsqueeze(1).broadcast_to([128, 4, 128]), ALU.mult, (rb, res("cst")), (res("wst"),))
                if M["use_saved"][l]:
                    cp("dve", KD[:, :, 0:128], KS[:, l], (res(f"ks{l}"),), (res("kd"),))
                    cp("dve", VD[:, 0], VS[:, l], (res(f"vs{l}"),), (res("vd"),))
                    cp("dve", ZC[:, :, 0:2], ZS[:, l], (res(f"zs{l}"),), (rZC,))
                if ns:
                    for g in range(8):
                        ts("dve", DG[:, g, :], ident[0:16, 0:16], PP[0:16, 124 + g:125 + g], None, ALU.mult, None,
                           (res("cst"), res("pp")), (res("dg"),))
                    R.dma("sp", a_in, STG[0:32, 0:1024], sconv[l], (), (res("stg"),))
                    pb, rb = bank()
                    for t in range(8):
                        tr(pb[:, t * 32:(t + 1) * 32], STG[0:32, t * 128:(t + 1) * 128], ident[0:32, 0:32],
                           (res("stg"), res("cst")), (rb,), t == 7)
                    cp("dve", SCT[:], pb[:, 0:256].rearrange("p (t c) -> p t c", c=32), (rb,), (res("sct"),))
                    odma(nc_s[l, :, 0, :], sconv[l].rearrange("(b j) c -> b j c", j=2)[:, 1, :], ())

                ckpt(f"params{mi}{l}")
                rmsnorm(l, 0, kv0, T)
                ckpt(f"norm{mi}{l}")

                def proj_fm(wb, rwb, nk, lhs_of, rhs_of, rhs_res, a, b, evac):
                    for (c0, c1) in chunks(a, b):
                        pb, rb = bank()
                        for k in range(nk):
                            mm(pb[:, 0:c1 - c0], lhs_of(k), rhs_of(k, c0, c1), k == 0, k == nk - 1,
                               (rwb,) + tuple(rhs_res(k)), (rb,), k == nk - 1)
                        evac(pb, rb, c0, c1)

                xn_rhs = lambda k, c0, c1: XN[:, k, c0:c1]
                xn_res = lambda k: (rXN[k],)

                pending = []
                hn = [0]

                def headnorm(pb, rb, c0, c1, gcol, out_ap, out_res, extra=None):
                    n = c1 - c0
                    tbi, rtbi = TB[hn[0] % 2], res(f"tb{hn[0] % 2}")
                    hn[0] += 1
                    act(tbi[:, 0:n], pb[:, 0:n], AF.Square, (rb,), (rtbi,))

                    def rest():
                        p2, r2 = bank()
                        mm(p2[:, 0:n], blk_bf, tbi[:, 0:n], True, True, (rtbi, res("cb")), (r2,), True)
                        act(T1[:, 0:n], p2[:, 0:n], AF.Ln, (r2,), (res("t1"),), bias=EPS, scale=1.0)
                        act(T1[:, 0:n], T1[:, 0:n], AF.Exp, (res("t1"),), (res("t1"),), scale=-0.5)
                        stt(out_ap, pb[:, 0:n], gcol, T1[:, 0:n], ALU.mult, ALU.mult, (rb, res("t1"), res("pp"), res("gq8")), (out_res,))
                        if extra is not None:
                            extra(pb, rb)
                    while pending:
                        pending.pop(0)()
                    pending.append(rest)

                def hn_flush():
                    while pending:
                        pending.pop(0)()

                for j in (() if M["kvonly"] else range(4)):
                    wb, rwb = get(("in", l, j * 256, 256))
                    wv = wb[:, 0:4096].rearrange("p (k c) -> p k c", c=256)
                    for t2 in range(2):
                        t = j * 2 + t2
                        proj_fm(wb, rwb, 16, lambda k, wv=wv, t2=t2: wv[:, k, t2 * 128:(t2 + 1) * 128], xn_rhs, xn_res, f0, T,
                                lambda pb, rb, c0, c1, t=t: headnorm(pb, rb, c0, c1, GQ8[:, 0:1], QN[:, t, c0 - f0:c1 - f0], rQN))
                for kp in range(2):
                    wb, rwb = get(("kd", l, kp))
                    wv = wb[:, 0:4096].rearrange("p (k c) -> p k c", c=256)
                    for v2 in range(2):
                        kvh = kp * 2 + v2

                        def kextra(pb, rb, kvh=kvh):
                            pass
                        def kev(pb, rb, c0, c1, kvh=kvh):
                            def ex(pb, rb, c0=c0, c1=c1, kvh=kvh):
                                if ns:
                                    lo = max(c0, Tp - 128)
                                    if lo < c1:
                                        stt(KOF[:, kvh, lo - (Tp - 128):c1 - (Tp - 128)], pb[:, lo - c0:c1 - c0], PP[:, 81:82], T1[:, lo - c0:c1 - c0],
                                            ALU.mult, ALU.mult, (rb, res("t1"), res("pp")), (res("kof"),))
                            headnorm(pb, rb, c0, c1, PP[:, 81:82], KD[:, kvh, 128 + c0:128 + c1], res("kd"), ex)
                        proj_fm(wb, rwb, 16, lambda k, wv=wv, v2=v2: wv[:, k, v2 * 128:(v2 + 1) * 128], xn_rhs, xn_res, kv0, T, kev)
                hn_flush()
                ckpt(f"qk{mi}{l}")
                wb, rwb = get(("in", l, 1280, 256))
                wv = wb[:, 0:4096].rearrange("p (k c) -> p k c", c=256)
                vblocks = [(c, 128) for c in range(kv0, Tp, 128)] + ([(Tp, 16)] if ns else [])
                for (c, P) in vblocks:
                    pb, rb = bank()
                    for k in range(16):
                        mm(pb[0:P, 0:256], XN[:, k, c:c + P], wv[:, k, :], k == 0, k == 15, (rwb, rXN[k]), (rb,), k == 15)
                    blk = 1 + c // 128
                    pv = pb[0:P, 0:256].rearrange("p (v d) -> p v d", d=64)
                    cp("act", VD[0:P, blk, :, 0:64], pv, (rb,), (res("vd"),))
                    cp("dve", VD[0:P, blk, :, 64:128], pv, (rb,), (res("vd"),))
                    if ns and c == Tp - 128:
                        cp("act", VOF[:, 0, :], pb[:, 0:256], (rb,), (res("vof"),))
                    if ns and c == Tp:
                        cp("act", VOF[0:16, 1, :], pb[0:16, 0:256], (rb,), (res("vof"),))
                if M["save"]:
                    cp("dve", KS[:, l], KD[:, :, Tp:Tp + 128], (res("kd"),), (res(f"ks{l}"),))
                    cp("dve", VS[:, l], VD[:, Tp // 128], (res("vd"),), (res(f"vs{l}"),))
                if M["kvonly"]:
                    for base in (2560, 3584):
                        for j in range(4):
                            wb, rwb = get(("in", l, base + j * 256, 256))
                            wv = wb[:, 0:4096].rearrange("p (k c) -> p k c", c=256)
                            for t2 in range(2):
                                t = j * 2 + t2
                                if base == 2560:
                                    ev = lambda pb, rb, c0, c1, t=t: cp("act", QN[:, t, 0:2], pb[:, 0:2], (rb,), (rQN,))
                                else:
                                    ev = lambda pb, rb, c0, c1, t=t: tt("dve", ZS[:, l, t, :], pb[:, 0:2], QN[:, t, 0:2], ALU.mult,
                                                                      (rb, rQN), (res(f"zs{l}"),))
                                proj_fm(wb, rwb, 16, lambda k, wv=wv, t2=t2: wv[:, k, t2 * 128:(t2 + 1) * 128], xn_rhs, xn_res, T - 2, T, ev)
                    continue
                ckpt(f"v{mi}{l}")
                def mgres(o0, o1):
                    return tuple(rMG[i] for i in range((o0 - 10496) // 656, (o1 - 1 - 10496) // 656 + 1))
                T3B = AR[:, 10496:11520].bitcast(F32)
                T2B = AR[:, 11808:12832].bitcast(F32)
                t3s = [(T3[:, 0:512], (res("t3"),)), (T3B[:, 0:512], mgres(10496, 11520))]
                t2s = [(T2, (res("t2"),)), (T2B, mgres(11808, 12832))]
                pts = [[(PT[0], (rPT[0],)), (PT[1], (rPT[1],))],
                       [(AR[:, 12832:13344], mgres(12832, 13344)), (AR[:, 13344:13856], mgres(13344, 13856))]]
                units = [(c, kvh) for c in range(f0, Tp, 128) for kvh in range(4)]

                def stage_a(u):
                    c, kvh = units[u]
                    first = c < M["first_lim"]
                    for kb in range(2):
                        kc = c - 128 + kb * 128
                        sbe, rse = bank_at(kb * 2)
                        sbo, rso = bank_at(kb * 2 + 1)
                        for g in range(4):
                            h = kvh * 4 + g
                            hp = slice(64 * (h % 2), 64 * (h % 2) + 64)
                            bk_, rk_ = (sbe, rse) if g % 2 == 0 else (sbo, rso)
                            mm(bk_[:, (g // 2) * 128:(g // 2 + 1) * 128], KD[hp, kvh, 128 + kc:128 + kc + 128],
                               QN[hp, h // 2, c - f0:c - f0 + 128], True, True, (res("kd"), rQN), (rk_,), g >= 2)
                        tabv = BT[:, kb, kvh * 4:kvh * 4 + 4, :]
                        t3a, t3r = t3s[kb]
                        t3v = t3a.rearrange("p (g q) -> p g q", q=128)
                        for par, (bk_, rk_) in enumerate(((sbe, rse), (sbo, rso))):
                            sv = bk_[:, 0:256].rearrange("p (g q) -> p g q", q=128)
                            if kb == 0 and first:
                                for g2_ in range(2):
                                    stt(t3v[:, par + 2 * g2_, :], sv[:, g2_, :], PP[:, 132:133], tabv[:, par + 2 * g2_, :], ALU.add, ALU.add,
                                        (rk_, res("bt"), res("pp")), t3r)
                            else:
                                tt("dve", t3v[:, par::2, :], sv, tabv[:, par::2, :], ALU.add, (rk_, res("bt")), t3r)
                        pt, rpt = pts[u % 2][kb]
                        act(pt, t3a, AF.Exp, t3r, rpt)

                def stage_b(u):
                    c, kvh = units[u]
                    ob_, rob = bank_at(4 + 2 * (u % 2))
                    db_, rdb = bank_at(5 + 2 * (u % 2))
                    for kb in range(2):
                        kc = c - 128 + kb * 128
                        pt, rpt = pts[u % 2][kb]
                        mm(ob_[:, :], VD[:, 1 + kc // 128, kvh, :], pt, kb == 0, kb == 1, (res("vd"),) + rpt, (rob,), kb == 1)
                        mm(db_[:, :], ones_bf, pt, kb == 0, kb == 1, (res("cb"),) + rpt, (rdb,), kb == 1)

                def stage_c(u):
                    c, kvh = units[u]
                    ob_, rob = bank_at(4 + 2 * (u % 2))
                    db_, rdb = bank_at(5 + 2 * (u % 2))
                    t2a, t2r = t2s[u % 2]
                    for g in range(4):
                        h = kvh * 4 + g
                        act(t2a[:, g * 128:(g + 1) * 128], db_[:, g * 128:(g + 1) * 128], AF.Ln, (rdb, res("esk")), t2r,
                            bias=ESK[:, h:h + 1])
                    act(t2a[:, 0:512], t2a[:, 0:512], AF.Exp, t2r, t2r, scale=-1.0)
                    for par in range(2):
                        hp = slice(64 * par, 64 * par + 64)
                        tt("dve", OB[hp, 2 * kvh:2 * kvh + 2, c - f0:c - f0 + 128],
                           ob_[hp, :].rearrange("p (t e q) -> p t e q", e=2, q=128)[:, :, par, :],
                           t2a[hp, 0:512].rearrange("p (t e q) -> p t e q", e=2, q=128)[:, :, par, :],
                           ALU.mult, (rob,) + t2r, (rOB,))

                if units:
                    stage_a(0)
                    for u in range(len(units)):
                        stage_b(u)
                        if u + 1 < len(units):
                            stage_a(u + 1)
                        stage_c(u)
                ckpt(f"pattn{mi}{l}")
                if ns:
                    sc = Tp - f0
                    sbe, rse = bank(pin=True)
                    sbo, rso = bank(pin=True)
                    kcts = [(KCT[0][:], (res("kct"),)),
                            (AR[:, 13856:14368].rearrange("p (v c) -> p v c", c=128), mgres(13856, 14368))]

                    def s_load(b):
                        s2 = b % 2
                        ckd = CKD[s2]
                        kct, rkct = kcts[s2]
                        R.dma("pool", a_ck[s2], [(ckd[:, :, u * 64:(u + 1) * 64], ck[l, b]) for u in range(2)], None, (), (res(f"ckd{s2}"),))
                        tpb, rtp = bank()
                        tpv = tpb[:, :].bitcast(BF16)
                        for kvh in range(4):
                            tr(tpv[:, kvh * 128:(kvh + 1) * 128], ckd[:, kvh, :], ident_bf, (res(f"ckd{s2}"), res("cb")), (rtp,), kvh == 3)
                        cp("act", kct, tpv[:, 0:512].rearrange("p (v c) -> p v c", c=128), (rtp,), rkct)

                    def s_mm(b):
                        kct, rkct = kcts[b % 2]
                        for h in range(16):
                            hp = slice(64 * (h % 2), 64 * (h % 2) + 64)
                            bk_, rk_ = (sbe, rse) if h % 2 == 0 else (sbo, rso)
                            mm(bk_[:, (h // 2) * 16 + b:(h // 2) * 16 + b + 1], kct[hp, h // 4, :], QN[hp, h // 2, sc + b:sc + b + 1], True, True,
                               rkct + (rQN,), (rk_,), h >= 14)

                    s_load(0)
                    for b in range(16):
                        if b + 1 < 16:
                            s_load(b + 1)
                        s_mm(b)
                    pinned.clear()
                    for par, (bk_, rk_) in enumerate(((sbe, rse), (sbo, rso))):
                        tt("dve", T3[:, 0:256].rearrange("p (t e b) -> p t e b", e=2, b=16)[:, :, par, :],
                           bk_[:, 0:128].rearrange("p (t b) -> p t b", b=16),
                           BSM[:, 0:256].rearrange("p (t e b) -> p t e b", e=2, b=16)[:, :, par, :], ALU.add, (rk_, res("bsm")), (res("t3"),))
                    act(PT[0][:, 0:256], T3[:, 0:256], AF.Exp, (res("t3"),), (rPT[0],))
                    s2e, r2e = bank()
                    s2o, r2o = bank()
                    for h in range(16):
                        hp = slice(64 * (h % 2), 64 * (h % 2) + 64)
                        bk_, rk_ = (s2e, r2e) if h % 2 == 0 else (s2o, r2o)
                        mm(bk_[0:16, (h // 2) * 16:(h // 2 + 1) * 16], KD[hp, h // 4, 128 + Tp:128 + T], QN[hp, h // 2, sc:sc + 16], True, True,
                           (res("kd"), rQN), (rk_,), h >= 14)
                    for par, (bk_, rk_) in enumerate(((s2e, r2e), (s2o, r2o))):
                        tt("dve", T3[0:16, 256:512].rearrange("p (t e b) -> p t e b", e=2, b=16)[:, :, par, :],
                           bk_[0:16, 0:128].rearrange("p (t b) -> p t b", b=16),
                           BSM[0:16, 256:512].rearrange("p (t e b) -> p t e b", e=2, b=16)[:, :, par, :], ALU.add, (rk_, res("bsm")), (res("t3"),))
                    act(PT[1][0:16, 0:256], T3[0:16, 256:512], AF.Exp, (res("t3"),), (rPT[1],))
                    ob_, rob = bank()
                    db_, rdb = bank()
                    mm(db_[:, 0:256], ones_bf, PT[0][:, 0:256], True, False, (res("cb"), rPT[0]), (rdb,), False)
                    mm(db_[:, 0:256], ones_bf[0:16, :], PT[1][0:16, 0:256], False, True, (res("cb"), rPT[1]), (rdb,), True)
                    first = True
                    for b in range(16):
                        s2 = b % 2
                        cvd = CVD[s2]
                        R.dma("pool", a_cv[s2], [(cvd[:, :, u * 64:(u + 1) * 64], cv[l, b]) for u in range(2)], None, (), (res(f"cvd{s2}"),))
                        for kvh in range(4):
                            cols = slice(kvh * 64 + b, kvh * 64 + b + 49, 16)
                            mm(ob_[:, cols], cvd[:, kvh, :], PT[0][:, cols], first, False, (res(f"cvd{s2}"), rPT[0]), (rob,), kvh == 3, skip=True)
                            first = False
                    for kvh in range(4):
                        mm(ob_[:, kvh * 64:(kvh + 1) * 64], VD[0:16, 1 + Tp // 128, kvh, :], PT[1][0:16, kvh * 64:(kvh + 1) * 64],
                           False, kvh == 3, (res("vd"), rPT[1]), (rob,), kvh == 3, skip=True)
                    for h in range(16):
                        act(T2[:, h * 16:(h + 1) * 16], db_[:, h * 16:(h + 1) * 16], AF.Ln, (rdb, res("esk")), (res("t2"),),
                            bias=ESK[:, h:h + 1])
                    act(T2[:, 0:256], T2[:, 0:256], AF.Exp, (res("t2"),), (res("t2"),), scale=-1.0)
                    for h in range(16):
                        hp = slice(64 * (h % 2), 64 * (h % 2) + 64)
                        tt("dve", OB[hp, h // 2, sc:sc + 16], ob_[hp, h * 16:(h + 1) * 16], T2[hp, h * 16:(h + 1) * 16],
                           ALU.mult, (rob, res("t2")), (rOB,))
                    pb, rb = bank()
                    for kvh in range(4):
                        tr(pb[:, kvh * 64:(kvh + 1) * 64], KOF[0:64, kvh, 0:128], ident[0:64, 0:64], (res("kof"), res("cst")), (rb,), False)
                    for kvh in range(4):
                        tr(pb[0:16, 256 + kvh * 64:256 + (kvh + 1) * 64], KOF[0:64, kvh, 128:144], ident[0:64, 0:64],
                           (res("kof"), res("cst")), (rb,), kvh == 3)
                    cp("dve", STG[:, 0:512], pb[:, 0:512], (rb,), (res("stg"),))
                    odma(nk_p[l], STG[:, 0:256], (res("stg"),))
                    odma(nk_s[l], STG[0:16, 256:512], (res("stg"),))
                    odma(nv_p[l], VOF[:, 0, :], (res("vof"),))
                    odma(nv_s[l], VOF[0:16, 1, :], (res("vof"),))

                def gate_proj(i_br):
                    for t in range(16):
                        wg, rwg = get(("gp", l, i_br, t))
                        wgv = wg[:, 0:2048].rearrange("p (k c) -> p k c", c=128)
                        wpv = wg[:, 2048:3072].rearrange("p (k c) -> p k c", c=128)
                        for (c0, c1) in chunks(f0, T):
                            n = c1 - c0
                            pg, rg = bank()
                            for k in range(16):
                                mm(pg[:, 0:n], wgv[:, k, :], XN[:, k, c0:c1], k == 0, k == 15, (rwg, rXN[k]), (rg,), k == 15)
                            gs = TB[1]
                            act(gs[:, 0:n], pg[:, 0:n], AF.Sigmoid, (rg, res("pp")), (res("tb1"),),
                                bias=PP[:, 32 + i_br * 16 + t:33 + i_br * 16 + t])
                            pq, rq = bank()
                            for k in range(8):
                                mm(pq[:, 0:n], wpv[:, k, :], OB[:, k, c0 - f0:c1 - f0], k == 0, k == 7, (rwg, rOB), (rq,), k == 7)
                            if i_br == 0:
                                tt("dve", MG[:, t, c0 - f0:c1 - f0], pq[:, 0:n], gs[:, 0:n], ALU.mult, (rq, res("tb1")), (rMG[t],))
                            else:
                                tt("dve", T3[:, 0:n], pq[:, 0:n], gs[:, 0:n], ALU.mult, (rq, res("tb1")), (res("t3"),))
                                tt("dve", MG[:, t, c0 - f0:c1 - f0], MG[:, t, c0 - f0:c1 - f0], T3[:, 0:n], ALU.add,
                                   (rMG[t], res("t3")), (rMG[t],))

                ckpt(f"attn{mi}{l}")
                gate_proj(0)
                ckpt(f"gate0{mi}{l}")

                z0 = max(f0 - 2, kv0)
                CG = QN
                for j in range(4):
                    wb, rwb = get(("in", l, 2560 + j * 256, 256))
                    wv = wb[:, 0:4096].rearrange("p (k c) -> p k c", c=256)
                    for t2 in range(2):
                        t = j * 2 + t2
                        proj_fm(wb, rwb, 16, lambda k, wv=wv, t2=t2: wv[:, k, t2 * 128:(t2 + 1) * 128], xn_rhs, xn_res, z0, T,
                                lambda pb, rb, c0, c1, t=t: cp("act", CG[:, t, c0 - z0:c1 - z0], pb[:, 0:c1 - c0], (rb,), (rQN,)))
                for j in range(4):
                    wb, rwb = get(("in", l, 3584 + j * 256, 256))
                    wv = wb[:, 0:4096].rearrange("p (k c) -> p k c", c=256)
                    for t2 in range(2):
                        t = j * 2 + t2
                        proj_fm(wb, rwb, 16, lambda k, wv=wv, t2=t2: wv[:, k, t2 * 128:(t2 + 1) * 128], xn_rhs, xn_res, z0, T,
                                lambda pb, rb, c0, c1, t=t: tt("dve", ZC[:, t, 2 + c0:2 + c1], pb[:, 0:c1 - c0], CG[:, t, c0 - z0:c1 - z0],
                                                             ALU.mult, (rb, rQN), (rZC,)))
                if M["save"]:
                    cp("dve", ZS[:, l], ZC[:, :, Tp:Tp + 2], (rZC,), (res(f"zs{l}"),))
                for j in range(4):
                    wb, rwb = get(("in", l, 1536 + j * 256, 256))
                    wv = wb[:, 0:4096].rearrange("p (k c) -> p k c", c=256)
                    for t2 in range(2):
                        t = j * 2 + t2

                        def cev(pb, rb, c0, c1, t=t):
                            e1 = min(c1, Tp)
                            n = e1 - c0
                            w0, w1, w2 = (PP[:, 84 + jj * 8 + t:85 + jj * 8 + t] for jj in range(3))
                            if n > 0:
                                ts("dve", T3[:, 0:n], ZC[:, t, c0:e1], w0, None, ALU.mult, None, (rZC, res("pp")), (res("t3"),))
                                stt(T3[:, 0:n], ZC[:, t, c0 + 1:e1 + 1], w1, T3[:, 0:n], ALU.mult, ALU.add, (rZC, res("t3"), res("pp")), (res("t3"),))
                                stt(T3[:, 0:n], ZC[:, t, c0 + 2:e1 + 2], w2, T3[:, 0:n], ALU.mult, ALU.add, (rZC, res("t3"), res("pp")), (res("t3"),))
                                tt("dve", OB[:, t, c0 - f0:e1 - f0], pb[:, 0:n], T3[:, 0:n], ALU.mult, (rb, res("t3")), (rOB,))
                            if ns and c1 > Tp:
                                sct = SCT[:, t, :].rearrange("p (b j) -> p j b", j=2)
                                o = Tp - c0
                                ts("dve", T3[:, 512:528], sct[:, 0, :], w0, None, ALU.mult, None, (res("sct"), res("pp")), (res("t3"),))
                                stt(T3[:, 512:528], sct[:, 1, :], w1, T3[:, 512:528], ALU.mult, ALU.add, (res("sct"), res("t3"), res("pp")), (res("t3"),))
                                stt(T3[:, 512:528], ZC[:, t, 2 + Tp:2 + T], w2, T3[:, 512:528], ALU.mult, ALU.add, (rZC, res("t3"), res("pp")), (res("t3"),))
                                tt("dve", OB[:, t, Tp - f0:T - f0], pb[:, o:o + 16], T3[:, 512:528], ALU.mult, (rb, res("t3")), (rOB,))
                        proj_fm(wb, rwb, 16, lambda k, wv=wv, t2=t2: wv[:, k, t2 * 128:(t2 + 1) * 128], xn_rhs, xn_res, f0, T, cev)
                if ns:
                    cp("dve", T1[:, 0:144].rearrange("p (t c) -> p t c", c=18)[:, :, 0:2], ZC[:, :, 2 + Tp - 2:2 + Tp], (rZC,), (res("t1"),))
                    cp("dve", T1[:, 0:144].rearrange("p (t c) -> p t c", c=18)[:, :, 2:18], ZC[:, :, 2 + Tp:2 + T], (rZC,), (res("t1"),))
                    pa, ra = bank()
                    pc, rc = bank()
                    for half, pbk, rbk in ((0, pa, ra), (1, pc, rc)):
                        for tt_ in range(4):
                            t = half * 4 + tt_
                            tr(pbk[0:18, tt_ * 128:(tt_ + 1) * 128], T1[:, t * 18:(t + 1) * 18], ident, (res("t1"), res("cst")), (rbk,), tt_ == 3)
                        cp("act", STG[0:18, half * 512:(half + 1) * 512], pbk[0:18, :], (rbk,), (res("stg"),))
                    odma(nc_p[l], STG[0:2, 0:1024], (res("stg"),))
                    odma(nc_s[l, :, 1, :], STG[2:18, 0:1024], (res("stg"),))
                ckpt(f"conv{mi}{l}")
                gate_proj(1)

                UT = QN
                for j in range(4):
                    wb, rwb = get(("in", l, 4608 + j * 256, 256))
                    wv = wb[:, 0:4096].rearrange("p (k c) -> p k c", c=256)
                    for t2 in range(2):
                        t = j * 2 + t2
                        proj_fm(wb, rwb, 16, lambda k, wv=wv, t2=t2: wv[:, k, t2 * 128:(t2 + 1) * 128], xn_rhs, xn_res, f0, T,
                                lambda pb, rb, c0, c1, t=t: gelu_from_psum(pb[:, 0:c1 - c0], UT[:, t, c0 - f0:c1 - f0], c1 - c0, 128, rb, rQN))
                gblocks = [(c, 128) for c in range(f0, Tp, 128)] + ([(Tp, 16)] if ns else [])
                for j in range(4):
                    wb, rwb = get(("in", l, 5632 + j * 256, 256))
                    wv = wb[:, 0:4096].rearrange("p (k c) -> p k c", c=256)
                    for bi, (c, P) in enumerate(gblocks):
                        pb, rb = bank()
                        for k in range(16):
                            mm(pb[0:P, 0:256], XN[:, k, c:c + P], wv[:, k, :], k == 0, k == 15, (rwb, rXN[k]), (rb,), k == 15)
                        gelu_from_psum(pb[0:P, 0:256], VGN[0:P, bi, j * 256:(j + 1) * 256], 256, P, rb, rZC)
                for bi, (c, P) in enumerate(gblocks):
                    v = VGN[0:P, bi, :]
                    R.op("act", lambda e, v=v, P=P: e.activation(out=STG[0:P, 0:1024], in_=v, func=AF.Square, accum_out=ST[0:P, 0:1]),
                         (rZC,), (res("stg"), res("st")))
                    act(ST[0:P, 1:2], ST[0:P, 0:1], AF.Ln, (res("st"),), (res("st"),), bias=EPS, scale=1.0 / 1024)
                    act(ST[0:P, 1:2], ST[0:P, 1:2], AF.Exp, (res("st"),), (res("st"),), scale=-0.5)
                    stt(v, v, ST[0:P, 1:2], VNB[0:P, :], ALU.mult, ALU.mult, (rZC, res("st"), res("vnb")), (rZC,))
                    if P == 16:
                        cp("dve", STG[0:16, 0:1024], v, (rZC,), (res("stg"),))
                        odma(ng_s[l], STG[0:16, 0:1024], (res("stg"),))
                    for g2 in range(2):
                        pb, rb = bank()
                        for jg in range(4):
                            g = g2 * 4 + jg
                            if P == 128:
                                mm(pb[:, jg * 128:(jg + 1) * 128], VGN[:, bi, g * 128:(g + 1) * 128], WST[:, g, :], True, False,
                                   (rZC, res("wst")), (rb,), False, skip=True)
                                mm(pb[:, jg * 128:(jg + 1) * 128], ones_bf[0:1, :], BSRB[0:1, g * 128:(g + 1) * 128], False, True,
                                   (res("cb"), res("bsrb")), (rb,), jg == 3, skip=True)
                            else:
                                mm(pb[:, jg * 16:(jg + 1) * 16], VGN[0:16, bi, g * 128:(g + 1) * 128], DG[:, g, :], True, False,
                                   (rZC, res("dg")), (rb,), False, skip=True)
                                mm(pb[:, jg * 16:(jg + 1) * 16], ones_bf[0:1, :], BSRB[0:1, 1024 + g * 16:1024 + (g + 1) * 16], False, True,
                                   (res("cb"), res("bsrb")), (rb,), jg == 3, skip=True)
                        tt("dve", OB[:, g2 * 4:g2 * 4 + 4, c - f0:c - f0 + P], pb[:, 0:4 * P].rearrange("p (j c) -> p j c", c=P),
                           UT[:, g2 * 4:g2 * 4 + 4, c - f0:c - f0 + P], ALU.mult, (rb, rQN), (rOB,))
                ckpt(f"gmlp{mi}{l}")
                gate_proj(2)

                for j in range(8):
                    wb, rwb = get(("out", l, j * 256))
                    wv = wb[:, 0:4096].rearrange("p (k c) -> p k c", c=256)
                    for t2 in range(2):
                        t = j * 2 + t2
                        proj_fm(wb, rwb, 16, lambda k, wv=wv, t2=t2: wv[:, k, t2 * 128:(t2 + 1) * 128],
                                lambda k, c0, c1: MG[:, k, c0 - f0:c1 - f0], lambda k: (rMG[k],), f0, T,
                                lambda pb, rb, c0, c1, t=t: tt("dve", X[:, t, c0:c1], X[:, t, c0:c1], pb[:, 0:c1 - c0], ALU.add,
                                                             (rb, rX[t]), (rX[t],)))

                ckpt(f"mixer{mi}{l}")
                R.barrier()
                rmsnorm(l, 16, f0, T)
                rHT = [res(f"ht{m}") for m in range(44)]
                for m in range(44):
                    wg, rwg = get(("gu", l, m))
                    wgv = wg[:, 0:2048].rearrange("p (k c) -> p k c", c=128)
                    wuv = wg[:, 2048:4096].rearrange("p (k c) -> p k c", c=128)
                    for (c0, c1) in chunks(f0, T):
                        n = c1 - c0
                        pg, rg = bank()
                        for k in range(16):
                            mm(pg[:, 0:n], wgv[:, k, :], XN[:, k, c0:c1], k == 0, k == 15, (rwg, rXN[k]), (rg,), k == 15)
                        gs = TB[1]
                        act(gs[:, 0:n], pg[:, 0:n], AF.Silu, (rg,), (res("tb1"),))
                        pu, ru = bank()
                        for k in range(16):
                            mm(pu[:, 0:n], wuv[:, k, :], XN[:, k, c0:c1], k == 0, k == 15, (rwg, rXN[k]), (ru,), k == 15)
                        tt("dve", HT[:, m, c0 - f0:c1 - f0], pu[:, 0:n], gs[:, 0:n], ALU.mult, (ru, res("tb1")), (rHT[m],))
                for ct in range(16):
                    cks = chunks(f0, T)
                    bks = [bank() for _ in cks]
                    for hf in range(2):
                        w_, rw_ = get(("dn", l, hf, ct))
                        wv_ = w_[:, 0:2816].rearrange("p (k c) -> p k c", c=128)
                        for (c0, c1), (pb, rb) in zip(cks, bks):
                            n = c1 - c0
                            for mm_ in range(22):
                                m = hf * 22 + mm_
                                mm(pb[:, 0:n], wv_[:, mm_, :], HT[:, m, c0 - f0:c1 - f0], m == 0, m == 43,
                                   (rw_, rHT[m]), (rb,), mm_ == 21)
                    for (c0, c1), (pb, rb) in zip(cks, bks):
                        n = c1 - c0
                        tt("dve", X[:, ct, c0:c1], X[:, ct, c0:c1], pb[:, 0:n], ALU.add, (rb, rX[ct]), (rX[ct],))
                ckpt(f"ffn{mi}{l}")
            R.barrier()
            if M["own0"] is None:
                continue
            oblocks = [(c, 128) for c in range(M["own0"], Tp, 128)] + ([(Tp, 16)] if ns else [])
            for oi, (c, P) in enumerate(oblocks):
                sg, rsg = STGS[oi % 3], res(f"stga{oi % 3}")
                for q4 in range(4):
                    pb, rb = bank()
                    for jj in range(4):
                        k = q4 * 4 + jj
                        tr(pb[0:P, jj * 128:(jj + 1) * 128], X[:, k, c:c + P], ident, (rX[k], res("cst")), (rb,), jj == 3)
                    cp("act" if q4 % 2 else "dve", sg[0:P, q4 * 512:(q4 + 1) * 512], pb[0:P, :], (rb,), (rsg,))
                if P == 128:
                    row = (c - M["own0"]) + M["yrow0"]
                    odma(y_p[row:row + 128, :], sg[:, :], (rsg,))
                else:
                    odma(y_s[:, :], sg[0:16, :], (rsg,))

        try:
            main_body()
        except Stop:
            R.barrier()
            a_dbg = R.dma_agent("dbg")
            out_agents.append(a_dbg)
            R.dma("sp", a_dbg, dbg_x[:, :, :], X[:], (), ())
            R.dma("pool", a_dbg, [(dbg_xn[:, :, :], XN[:]), (dbg_ar.rearrange("p (m c) -> p m c", c=656), AR[:, :].rearrange("p (m c) -> p m c", c=656)), (dbg_kd[:, :, :], KD[:]),
                                  (dbg_vd[:, :, :, :], VD[:])], None, (), ())
        for a in out_agents:
            R._need("sp", (a, a.count))
        R.barrier()

        with nc.Block() as block:
            for e, attr in ENG.items():
                ops = R.ops[e]

                def body(eng, ops=ops):
                    for f in ops:
                        f(eng)
                getattr(block, attr)(body)
    return nc


_NC = None


def _tables():
    slopes = np.exp2(-8.0 * np.arange(1, 17, dtype=np.float32) / 16).astype(np.float32)
    k = np.arange(128)[:, None]
    q = np.arange(128)[None, :]
    bt = np.zeros((2, 128, 16, 128), np.float32)
    for h in range(16):
        dprev = (q - k + 128).astype(np.float32)
        bt[0, :, h, :] = np.where(k > q, -slopes[h] * dprev, NEG)
        down = (q - k).astype(np.float32)
        bt[1, :, h, :] = np.where(k <= q, -slopes[h] * down, NEG)
    cst = np.zeros((128, 512), np.float32)
    cst[:, 0:128] = np.eye(128, dtype=np.float32)
    blk = np.zeros((128, 128), np.float32)
    blk[0:64, 0:64] = 1.0 / 64
    blk[64:128, 64:128] = 1.0 / 64
    cst[:, 128:256] = blk
    cst[:, 256:384] = 1.0
    cst[:, 384:512] = (q >= k).astype(np.float32)
    bsm = np.zeros((128, 512), np.float32)
    j = np.arange(128, dtype=np.float32)
    for h in range(16):
        col = np.where(j >= 1, -slopes[h] * (128.0 - j), NEG)
        bsm[:, h * 16:(h + 1) * 16] = col[:, None]
    m = np.full((16, 16), NEG, np.float32)
    np.fill_diagonal(m, 0.0)
    bsm[0:16, 256:512] = np.tile(m, (1, 16))
    return bt, cst, bsm


def kernel(x_prompt, x_sample, cache_k, cache_v, state_conv, norm_mix, w_in, b_gate, q_norm, k_norm,
           sinks, conv_w, v_norm, w_spatial, b_spatial, w_branch, w_out, norm_ffn, w_gate_up, w_down):
    global _NC
    f = lambda a: np.ascontiguousarray(np.asarray(a, dtype=np.float32))
    x_prompt, x_sample, cache_k, cache_v, state_conv = map(f, (x_prompt, x_sample, cache_k, cache_v, state_conv))
    w_in, w_branch, w_out, w_gate_up, w_down = map(f, (w_in, w_branch, w_out, w_gate_up, w_down))
    norm_mix, b_gate, q_norm, k_norm, sinks, conv_w, v_norm, w_spatial, b_spatial, norm_ffn = map(
        f, (norm_mix, b_gate, q_norm, k_norm, sinks, conv_w, v_norm, w_spatial, b_spatial, norm_ffn))
    if _NC is None:
        _NC = build()
    bt, cst, bsm = _tables()
    pp = np.zeros((L, 128, 144), np.float32)
    for l in range(L):
        pp[l, :, 0:16] = norm_mix[l].reshape(16, 128).T
        pp[l, :, 16:32] = norm_ffn[l].reshape(16, 128).T
        for i in range(3):
            pp[l, :, 32 + i * 16:48 + i * 16] = b_gate[l, i].reshape(16, 128).T
        pp[l, :, 80] = np.tile(q_norm[l], 2)
        pp[l, :, 81] = np.tile(k_norm[l], 2)
        for jj in range(3):
            pp[l, :, 84 + jj * 8:92 + jj * 8] = conv_w[l, jj].reshape(8, 128).T
        pp[l, :, 108:124] = sinks[l][None, :]
        pp[l, :, 124:132] = w_spatial[l, :, 0, 0][None, :]
    vnb = np.ascontiguousarray(np.broadcast_to(v_norm[:, None, :], (L, 128, 1024)))
    wsp = np.ascontiguousarray(w_spatial.transpose(0, 2, 1, 3))
    bsr = np.zeros((L, 1152), np.float32)
    bsr[:, 0:1024] = b_spatial.reshape(L, 1024)
    bsr[:, 1024:1152] = np.repeat(b_spatial[:, :, 0], 16, axis=1)
    in_maps = []
    for c in range(8):
        b, h = c // 2, c % 2
        xin = np.zeros((1280, D), np.float32)
        if h == 0:
            xin[256:] = x_prompt[b, 0:1024]
        else:
            xin[:] = x_prompt[b, 768:2048]
        ppc = pp.copy()
        ppc[:, :, 132] = 0.0 if h == 1 else NEG
        sl = slice(c * 16, (c + 1) * 16)
        in_maps.append(dict(
            xin=xin, xsm=f(x_sample[sl, 0, :]), ck=f(cache_k[:, sl]), cv=f(cache_v[:, sl]),
            sconv=f(state_conv[:, sl].reshape(L, 32, 1024)),
            w_in=w_in, w_br=w_branch, w_out=w_out, w_gu=w_gate_up, w_dn=w_down,
            pp=ppc, vnb=vnb, wsp=wsp, bsr=bsr, cst=cst, btab=bt, bsm=bsm))
    res = run_bass_kernel_spmd(_NC, in_maps, core_ids=list(range(8)))
    r = res.results
    y_prompt = np.zeros((4, 2048, D), np.float32)
    nkp = np.zeros((L, 4, 128, 4, 64), np.float32)
    nvp = np.zeros((L, 4, 128, 4, 64), np.float32)
    ncp = np.zeros((L, 4, 2, 1024), np.float32)
    for c in range(8):
        b, h = c // 2, c % 2
        y_prompt[b, h * 1024:(h + 1) * 1024] = r[c]["y_p"]
        if h == 1:
            nkp[:, b] = r[c]["nk_p"].reshape(L, 128, 4, 64)
            nvp[:, b] = r[c]["nv_p"].reshape(L, 128, 4, 64)
            ncp[:, b] = r[c]["nc_p"]
    y_sample = np.concatenate([r[c]["y_s"] for c in range(8)], 0).reshape(128, 1, D)
    nks = np.concatenate([r[c]["nk_s"] for c in range(8)], 1).reshape(L, 128, 1, 4, 64)
    nvs = np.concatenate([r[c]["nv_s"] for c in range(8)], 1).reshape(L, 128, 1, 4, 64)
    ncs = np.concatenate([r[c]["nc_s"] for c in range(8)], 1)
    ngs = np.concatenate([r[c]["ng_s"] for c in range(8)], 1).reshape(L, 128, 1, 1024)
    return (y_prompt, y_sample, nkp, nvp, ncp, nks, nvs, ncs, ngs)
```
